# Optimizing a Trainium2 kernel written in Bass

```python
import jax
import jax.numpy as jnp
from jax import lax
import numpy as np

D_MODEL = 1024
BATCH = 8
SEQ = 4096
DEPTH = 4

CTX_LEN = 256
GRID_W = 64
N_BRANCHES = 3
NA_HEADS = 8
NA_HEAD_DIM = 64
NA_WIDTH = NA_HEADS * NA_HEAD_DIM
NA_WIN_H = 8
NA_WIN_W = 16
GM_GROUPS = 4
GM_CHUNK = 128
GM_GROUP_DIM = 128
GM_WIDTH = GM_GROUPS * GM_GROUP_DIM
MLA_HEADS = 8
MLA_Q_RANK = 256
MLA_KV_RANK = 128
MLA_NOPE = 64
MLA_ROPE = 32
MLA_V = 64
MLA_QK = MLA_NOPE + MLA_ROPE
ROPE_AXIS_PAIRS = MLA_ROPE // 4
ROPE_THETA = 10000.0
FFN_HIDDEN = 4 * D_MODEL
Q_BLOCK = 128
NORM_EPS = 1e-6
NEG_INF = -1e30
MOD_SCALE = 0.5

OFF_NA_K = 0
OFF_NA_V = OFF_NA_K + NA_WIDTH
OFF_MLA_CKV = OFF_NA_V + NA_WIDTH
OFF_MLA_KR = OFF_MLA_CKV + MLA_KV_RANK
N_CTX_COLS = OFF_MLA_KR + MLA_ROPE
OFF_NA_Q = N_CTX_COLS
OFF_GM_U = OFF_NA_Q + NA_WIDTH
OFF_GM_V = OFF_GM_U + GM_WIDTH
OFF_MLA_CQ = OFF_GM_V + GM_WIDTH
OFF_GATES = OFF_MLA_CQ + MLA_Q_RANK
N_IN_COLS = OFF_GATES + N_BRANCHES * D_MODEL

kernel_name = 'hybrid_na_gmlp_mla_dit_block'


def rms_norm(t, gain):
    tf = t.astype(jnp.float32)
    y = tf * lax.rsqrt(jnp.mean(tf * tf, axis=-1, keepdims=True) + NORM_EPS)
    return (y * gain.astype(jnp.float32)).astype(t.dtype)


def layer_norm(t, gain, bias):
    tf = t.astype(jnp.float32)
    mu = jnp.mean(tf, axis=-1, keepdims=True)
    var = jnp.mean(jnp.square(tf - mu), axis=-1, keepdims=True)
    y = (tf - mu) * lax.rsqrt(var + NORM_EPS)
    return (y * gain.astype(jnp.float32) + bias.astype(jnp.float32)).astype(t.dtype)


def modulate(t, shift, scale):
    return t * (1 + scale) + shift


def cols(t, off, width):
    return t[..., off:off + width]


def split_heads(t, n_heads):
    b, n, w = t.shape
    return t.reshape(b, n, n_heads, w // n_heads).transpose(0, 2, 1, 3)


def merge_heads(t):
    b, h, n, d = t.shape
    return t.transpose(0, 2, 1, 3).reshape(b, n, h * d)


def axial_rope(n_tokens):
    t = jnp.arange(n_tokens)
    rows = (t // GRID_W).astype(jnp.float32)
    colsv = (t % GRID_W).astype(jnp.float32)
    freqs = ROPE_THETA ** (-jnp.arange(ROPE_AXIS_PAIRS, dtype=jnp.float32) / ROPE_AXIS_PAIRS)
    ang = jnp.stack([rows[:, None] * freqs, colsv[:, None] * freqs], axis=1)
    return jnp.cos(ang), jnp.sin(ang)


def apply_rope_tail(t, cos, sin):
    head, tail = t[..., :-MLA_ROPE], t[..., -MLA_ROPE:]
    tr = tail.reshape(tail.shape[:-1] + (2, 2, ROPE_AXIS_PAIRS)).astype(jnp.float32)
    x1, x2 = tr[..., 0, :], tr[..., 1, :]
    c, s = cos[:, None], sin[:, None]
    rot = jnp.stack([x1 * c - x2 * s, x2 * c + x1 * s], axis=-2).reshape(tail.shape).astype(t.dtype)
    return jnp.concatenate([head, rot], axis=-1)


def dense_attention(q, k, v):
    b, h, n, d = q.shape
    scale = d ** -0.5
    qb = jnp.moveaxis(q.reshape(b, h, n // Q_BLOCK, Q_BLOCK, d), 2, 0)

    def attend(q_blk):
        s = jnp.einsum('bhqd,bhkd->bhqk', q_blk, k).astype(jnp.float32) * scale
        p = jax.nn.softmax(s, axis=-1).astype(v.dtype)
        return jnp.einsum('bhqk,bhkd->bhqd', p, v)

    out = lax.map(attend, qb)
    return jnp.moveaxis(out, 0, 2).reshape(b, h, n, v.shape[-1])


def neighbourhood_attention(q, k, v, k_ctx, v_ctx, rpb):
    b, h, n, dh = q.shape
    rows = n // GRID_W
    kh = min(NA_WIN_H, rows)
    kw = NA_WIN_W
    scale = dh ** -0.5
    qg = jnp.moveaxis(q.reshape(b, h, rows, GRID_W, dh), 2, 0)
    kg = k.reshape(b, h, rows, GRID_W, dh)
    vg = v.reshape(b, h, rows, GRID_W, dh)
    r = jnp.arange(rows)
    r0 = jnp.clip(r - kh // 2, 0, rows - kh)
    cq = jnp.arange(GRID_W)
    c0 = jnp.clip(cq - kw // 2, 0, GRID_W - kw)
    col_in = (cq[None, :] >= c0[:, None]) & (cq[None, :] < c0[:, None] + kw)
    dc = jnp.clip(cq[None, :] - cq[:, None], -(NA_WIN_W - 1), NA_WIN_W - 1) + NA_WIN_W - 1

    def row_block(args):
        q_r, r_i, r0_i = args
        k_band = lax.dynamic_slice_in_dim(kg, r0_i, kh, axis=2)
        v_band = lax.dynamic_slice_in_dim(vg, r0_i, kh, axis=2)
        dr = r0_i + jnp.arange(kh) - r_i + NA_WIN_H - 1
        bias = rpb[:, dr[None, :, None], dc[:, None, :]].astype(jnp.float32)
        s_loc = jnp.einsum('bhqd,bhkcd->bhqkc', q_r, k_band).astype(jnp.float32) * scale + bias
        s_loc = jnp.where(col_in[:, None, :], s_loc, NEG_INF).reshape(b, h, GRID_W, kh * GRID_W)
        s_ctx = jnp.einsum('bhqd,bhld->bhql', q_r, k_ctx).astype(jnp.float32) * scale
        p = jax.nn.softmax(jnp.concatenate([s_loc, s_ctx], axis=-1), axis=-1).astype(v.dtype)
        p_loc = p[..., :kh * GRID_W].reshape(b, h, GRID_W, kh, GRID_W)
        return (jnp.einsum('bhqkc,bhkcd->bhqd', p_loc, v_band)
                + jnp.einsum('bhql,bhld->bhqd', p[..., kh * GRID_W:], v_ctx))

    out = lax.map(row_block, (qg, r, r0))
    return jnp.moveaxis(out, 0, 2).reshape(b, h, n, dh)


def spatial_gating(u, v, ln_g, ln_b, w_s, b_s):
    b, n, _ = v.shape
    v = layer_norm(v, ln_g, ln_b)
    vc = v.reshape(b, n // GM_CHUNK, GM_CHUNK, GM_GROUPS, GM_GROUP_DIM)
    mixed = jnp.einsum('gpq,bnqgc->bnpgc', w_s, vc) + b_s.T[:, :, None]
    return u * mixed.reshape(b, n, GM_WIDTH)


def mla_queries(c_q, cq_gain, w_uq, q_gain, rope):
    b, n, _ = c_q.shape
    q = (rms_norm(c_q, cq_gain) @ w_uq).reshape(b, n, MLA_HEADS, MLA_QK)
    q = rms_norm(q, q_gain)
    if rope is not None:
        q = apply_rope_tail(q, rope[0], rope[1])
    return q.transpose(0, 2, 1, 3)


def mla_keys_values(c_kv, k_rope, ckv_gain, w_ukv, k_gain, rope):
    b, n, _ = c_kv.shape
    kv = (rms_norm(c_kv, ckv_gain) @ w_ukv).reshape(b, n, MLA_HEADS, MLA_NOPE + MLA_V)
    k_nope, v = kv[..., :MLA_NOPE], kv[..., MLA_NOPE:]
    k_r = jnp.broadcast_to(k_rope[:, :, None, :], (b, n, MLA_HEADS, MLA_ROPE))
    k = rms_norm(jnp.concatenate([k_nope, k_r], axis=-1), k_gain)
    if rope is not None:
        k = apply_rope_tail(k, rope[0], rope[1])
    return k.transpose(0, 2, 1, 3), v.transpose(0, 2, 1, 3)


def merge_branches(y_na, y_gm, y_mla, gate_logits, na_w_o, gm_w_o, mla_w_o, w_out):
    g = jax.nn.sigmoid(gate_logits).reshape(gate_logits.shape[:-1] + (N_BRANCHES, D_MODEL))
    y = g[..., 0, :] * (y_na @ na_w_o) + g[..., 1, :] * (y_gm @ gm_w_o) + g[..., 2, :] * (y_mla @ mla_w_o)
    return y @ w_out


def squared_relu_mlp(h, w1, w2):
    return jnp.square(jax.nn.relu(h @ w1)) @ w2


def setup_inputs(seed: int = 0) -> dict:
    key = jax.random.key(seed)
    ks = jax.random.split(key, 28)
    f32 = jnp.float32

    def nrm(k, shape, scale):
        return scale * jax.random.normal(k, shape, f32)

    def gain(k, shape):
        return 1.0 + 0.02 * jax.random.normal(k, shape, f32)

    L = DEPTH
    return {
        'x': nrm(ks[0], (BATCH, SEQ, D_MODEL), 1.0),
        'c': nrm(ks[1], (BATCH, D_MODEL), 1.0),
        'ctx': nrm(ks[2], (BATCH, CTX_LEN, D_MODEL), 1.0),
        'c_ctx': nrm(ks[3], (D_MODEL,), 1.0),
        'w_mod': nrm(ks[4], (L, D_MODEL, 6 * D_MODEL), MOD_SCALE * D_MODEL ** -0.5),
        'b_mod': nrm(ks[5], (L, 6 * D_MODEL), 0.02),
        'g_norm1': gain(ks[6], (L, D_MODEL)),
        'g_norm2': gain(ks[7], (L, D_MODEL)),
        'w_in': nrm(ks[8], (L, D_MODEL, N_IN_COLS), D_MODEL ** -0.5),
        'na_q_gain': gain(ks[9], (L, NA_HEAD_DIM)),
        'na_k_gain': gain(ks[10], (L, NA_HEAD_DIM)),
        'na_rpb': nrm(ks[11], (L, NA_HEADS, 2 * NA_WIN_H - 1, 2 * NA_WIN_W - 1), 0.1),
        'na_w_o': nrm(ks[12], (L, NA_WIDTH, D_MODEL), NA_WIDTH ** -0.5),
        'gm_ln_g': gain(ks[13], (L, GM_WIDTH)),
        'gm_ln_b': nrm(ks[14], (L, GM_WIDTH), 0.02),
        'gm_w_s': nrm(ks[15], (L, GM_GROUPS, GM_CHUNK, GM_CHUNK), GM_CHUNK ** -0.5),
        'gm_b_s': gain(ks[16], (L, GM_GROUPS, GM_CHUNK)),
        'gm_w_o': nrm(ks[17], (L, GM_WIDTH, D_MODEL), GM_WIDTH ** -0.5),
        'mla_cq_gain': gain(ks[18], (L, MLA_Q_RANK)),
        'mla_ckv_gain': gain(ks[19], (L, MLA_KV_RANK)),
        'mla_w_uq': nrm(ks[20], (L, MLA_Q_RANK, MLA_HEADS * MLA_QK), MLA_Q_RANK ** -0.5),
        'mla_w_ukv': nrm(ks[21], (L, MLA_KV_RANK, MLA_HEADS * (MLA_NOPE + MLA_V)), MLA_KV_RANK ** -0.5),
        'mla_q_gain': gain(ks[22], (L, MLA_QK)),
        'mla_k_gain': gain(ks[23], (L, MLA_QK)),
        'mla_w_o': nrm(ks[24], (L, MLA_HEADS * MLA_V, D_MODEL), (MLA_HEADS * MLA_V) ** -0.5),
        'w_out': nrm(ks[25], (L, D_MODEL, D_MODEL), D_MODEL ** -0.5),
        'ffn_w1': nrm(ks[26], (L, D_MODEL, FFN_HIDDEN), D_MODEL ** -0.5),
        'ffn_w2': nrm(ks[27], (L, FFN_HIDDEN, D_MODEL), FFN_HIDDEN ** -0.5),
    }


def reference(x, c, ctx, c_ctx, w_mod, b_mod, g_norm1, g_norm2, w_in, na_q_gain, na_k_gain, na_rpb,
              na_w_o, gm_ln_g, gm_ln_b, gm_w_s, gm_b_s, gm_w_o, mla_cq_gain, mla_ckv_gain, mla_w_uq,
              mla_w_ukv, mla_q_gain, mla_k_gain, mla_w_o, w_out, ffn_w1, ffn_w2):
    rope = axial_rope(x.shape[1])
    s_c = jax.nn.silu(c)
    s_ctx = jax.nn.silu(c_ctx)[None]
    for i in range(DEPTH):
        last = i == DEPTH - 1
        mod = (s_c @ w_mod[i] + b_mod[i])[:, None, :]
        sh1, sc1, gt1, sh2, sc2, gt2 = jnp.split(mod, 6, axis=-1)
        n_mod_c = 2 * D_MODEL if last else 6 * D_MODEL
        mod_c = (s_ctx @ w_mod[i][:, :n_mod_c] + b_mod[i][:n_mod_c])[:, None, :]
        mc = jnp.split(mod_c, n_mod_c // D_MODEL, axis=-1)

        h = modulate(rms_norm(x, g_norm1[i]), sh1, sc1)
        hc = modulate(rms_norm(ctx, g_norm1[i]), mc[0], mc[1])
        p = h @ w_in[i]
        pc = hc @ w_in[i][:, :(N_CTX_COLS if last else N_IN_COLS)]

        k_na_c = rms_norm(split_heads(cols(pc, OFF_NA_K, NA_WIDTH), NA_HEADS), na_k_gain[i])
        v_na_c = split_heads(cols(pc, OFF_NA_V, NA_WIDTH), NA_HEADS)
        k_mla_c, v_mla_c = mla_keys_values(cols(pc, OFF_MLA_CKV, MLA_KV_RANK), cols(pc, OFF_MLA_KR, MLA_ROPE),
                                           mla_ckv_gain[i], mla_w_ukv[i], mla_k_gain[i], None)

        q_na = rms_norm(split_heads(cols(p, OFF_NA_Q, NA_WIDTH), NA_HEADS), na_q_gain[i])
        k_na = rms_norm(split_heads(cols(p, OFF_NA_K, NA_WIDTH), NA_HEADS), na_k_gain[i])
        v_na = split_heads(cols(p, OFF_NA_V, NA_WIDTH), NA_HEADS)
        y_na = merge_heads(neighbourhood_attention(q_na, k_na, v_na, k_na_c, v_na_c, na_rpb[i]))
        y_gm = spatial_gating(jax.nn.gelu(cols(p, OFF_GM_U, GM_WIDTH), approximate=False),
                              jax.nn.gelu(cols(p, OFF_GM_V, GM_WIDTH), approximate=False),
                              gm_ln_g[i], gm_ln_b[i], gm_w_s[i], gm_b_s[i])
        q_mla = mla_queries(cols(p, OFF_MLA_CQ, MLA_Q_RANK), mla_cq_gain[i], mla_w_uq[i], mla_q_gain[i], rope)
        k_mla, v_mla = mla_keys_values(cols(p, OFF_MLA_CKV, MLA_KV_RANK), cols(p, OFF_MLA_KR, MLA_ROPE),
                                       mla_ckv_gain[i], mla_w_ukv[i], mla_k_gain[i], rope)
        y_mla = merge_heads(dense_attention(q_mla, jnp.concatenate([k_mla_c, k_mla], axis=2),
                                            jnp.concatenate([v_mla_c, v_mla], axis=2)))
        mix = merge_branches(y_na, y_gm, y_mla, cols(p, OFF_GATES, N_BRANCHES * D_MODEL),
                             na_w_o[i], gm_w_o[i], mla_w_o[i], w_out[i])
        x_new = x + gt1 * mix
        x_new = x_new + gt2 * squared_relu_mlp(modulate(rms_norm(x_new, g_norm2[i]), sh2, sc2),
                                               ffn_w1[i], ffn_w2[i])

        if not last:
            q_na_c = rms_norm(split_heads(cols(pc, OFF_NA_Q, NA_WIDTH), NA_HEADS), na_q_gain[i])
            y_na_c = merge_heads(dense_attention(q_na_c, k_na_c, v_na_c))
            y_gm_c = spatial_gating(jax.nn.gelu(cols(pc, OFF_GM_U, GM_WIDTH), approximate=False),
                                    jax.nn.gelu(cols(pc, OFF_GM_V, GM_WIDTH), approximate=False),
                                    gm_ln_g[i], gm_ln_b[i], gm_w_s[i], gm_b_s[i])
            q_mla_c = mla_queries(cols(pc, OFF_MLA_CQ, MLA_Q_RANK), mla_cq_gain[i], mla_w_uq[i],
                                  mla_q_gain[i], None)
            y_mla_c = merge_heads(dense_attention(q_mla_c, k_mla_c, v_mla_c))
            mix_c = merge_branches(y_na_c, y_gm_c, y_mla_c, cols(pc, OFF_GATES, N_BRANCHES * D_MODEL),
                                   na_w_o[i], gm_w_o[i], mla_w_o[i], w_out[i])
            ctx = ctx + mc[2] * mix_c
            ctx = ctx + mc[5] * squared_relu_mlp(modulate(rms_norm(ctx, g_norm2[i]), mc[3], mc[4]),
                                                 ffn_w1[i], ffn_w2[i])
        x = x_new
    return x
```

```python
from contextlib import ExitStack
import numpy as np
import concourse.bass as bass
import concourse.mybir as mybir
from concourse.bass_utils import run_bass_kernel_spmd

F32 = mybir.dt.float32
BF16 = mybir.dt.bfloat16
AF = mybir.ActivationFunctionType
ALU = mybir.AluOpType
AX = mybir.AxisListType

D = 1024
SEQ = 4096
CL = 256
NT = SEQ + CL
NTILE = NT // 128
DEPTH = 4
EPS = 1e-6
GRID_W = 64
OFF_NA_K, OFF_NA_V, OFF_CKV, OFF_KR = 0, 512, 1024, 1152
OFF_NA_Q, OFF_GM_U, OFF_GM_V, OFF_CQ, OFF_GATES = 1184, 1696, 2208, 2720, 2976
N_IN = 6048
NTOK_COLS = 2976
V_GQNA, V_GKNA, V_LNG, V_LNB, V_BS, V_GCQ, V_GCKV, V_GQM, V_GKM = 0, 64, 128, 640, 1152, 1664, 1920, 2048, 2144
NVEC = 2240


class Buf:
    __slots__ = ("name", "w", "r", "ds")

    def __init__(self, name=""):
        self.name = name
        self.w = None
        self.r = []
        self.ds = None


class DmaSem:
    __slots__ = ("sem", "tot")

    def __init__(self, sem):
        self.sem = sem
        self.tot = 0


class Prog:
    COMPUTE = ("pe", "act", "dve", "pool")

    def __init__(self, nc, same_engine_sync=("act", "dve", "pool")):
        self.nc = nc
        self.eng = {"pe": nc.tensor, "act": nc.scalar, "dve": nc.vector, "pool": nc.gpsimd, "sp": nc.sync}
        self.esem = {}
        self.ecnt = {}
        for e in self.COMPUTE:
            self.esem[e] = nc.alloc_semaphore(name="es_" + e)
            self.ecnt[e] = 0
        self.seen = {e: {} for e in self.eng}
        self.same = set(same_engine_sync)
        self.free_ds = []
        self.all_ds = []
        self.scope = []
        self.nins = 0
        self.nwait = 0

    def _wait(self, e, tok):
        if tok is None:
            return
        sem, val, owner = tok
        if owner == e and e not in self.same:
            return
        key = id(sem)
        if self.seen[e].get(key, 0) >= val:
            return
        self.eng[e].wait_ge(sem, val)
        self.seen[e][key] = val
        self.nwait += 1

    def _deps(self, e, reads, writes):
        for b in reads:
            self._wait(e, b.w)
        for b in writes:
            self._wait(e, b.w)
            for t in b.r:
                self._wait(e, t)

    def _commit(self, tok, reads, writes):
        for b in reads:
            b.r.append(tok)
            if len(b.r) > 48:
                last = {}
                for t in b.r:
                    k = id(t[0])
                    if k not in last or last[k][1] < t[1]:
                        last[k] = t
                b.r = list(last.values())
        for b in writes:
            b.w = tok
            b.r = []

    def op(self, e, fn, reads=(), writes=(), signal=True):
        self._deps(e, reads, writes)
        ins = fn(self.eng[e])
        self.nins += 1
        if signal:
            self.ecnt[e] += 1
            ins.then_inc(self.esem[e], 1)
            tok = (self.esem[e], self.ecnt[e], e)
        else:
            tok = (self.esem[e], self.ecnt[e] + 1, e)
        self._commit(tok, reads, writes)
        return ins

    def dma(self, q, out_ap, in_ap, reads, writes, slot, **kw):
        if slot.ds is None:
            if self.free_ds:
                slot.ds = self.free_ds.pop()
            else:
                slot.ds = DmaSem(self.nc.alloc_semaphore(name="ds%d" % len(self.all_ds)))
                self.all_ds.append(slot.ds)
            self.scope.append(slot)
        ds = slot.ds
        if ds.tot:
            self._wait(q, (ds.sem, ds.tot, "dma"))
        self._deps(q, reads, writes)
        ins = self.eng[q].dma_start(out=out_ap, in_=in_ap, **kw)
        self.nins += 1
        ds.tot += 16
        ins.then_inc(ds.sem, 16)
        tok = (ds.sem, ds.tot, "dma")
        self._commit(tok, reads, writes)
        return ins

    def barrier(self, release=True):
        for e in self.eng:
            for e2 in self.COMPUTE:
                if e2 != e and self.ecnt[e2] > 0:
                    self._wait(e, (self.esem[e2], self.ecnt[e2], e2))
            for ds in self.all_ds:
                if ds.tot:
                    self._wait(e, (ds.sem, ds.tot, "dma"))
        if release:
            for b in self.scope:
                self.free_ds.append(b.ds)
                b.ds = None
            self.scope = []

    def finish(self, q="sp"):
        for ds in self.all_ds:
            if ds.tot:
                self._wait(q, (ds.sem, ds.tot, "dma"))


class Ring:
    def __init__(self, tensors):
        self.t = tensors
        self.b = [Buf() for _ in tensors]
        self.i = 0

    def next(self):
        i = self.i
        self.i = (i + 1) % len(self.t)
        return self.t[i], self.b[i]


class Kern:
    def __init__(self, nl, last_flags, debug=False):
        self.nl = nl
        self.last = last_flags
        self.debug = debug
        nc = self.nc = bass.Bass("TRN2", target_bir_lowering=False)
        self.P = Prog(nc)
        self.qi = 0
        di = lambda name, shape: nc.dram_tensor(name, list(shape), F32, kind="ExternalInput").ap()
        self.xin = di("xin", [NT, D])
        self.cc = di("cc", [128, 8, 2])
        self.fm = di("fm", [nl, 128, 64])
        self.bmrow = di("bmrow", [nl, 1, 6144])
        self.vecs = di("vecs", [nl, 128, NVEC])
        self.w_mod = di("w_mod", [nl, D, 6144])
        self.w_in = di("w_in", [nl, D, N_IN])
        self.rpbg = di("rpbg", [nl, 128, 8 * 16 * 64])
        self.m01 = di("m01", [128, 2 * 16 * 64])
        self.ropet = di("ropet", [NT, 64])
        self.wsT = di("wsT", [nl, 128, 512])
        self.w_uq = di("w_uq", [nl, 256, 768])
        self.w_ukv = di("w_ukv", [nl, 128, 1024])
        self.na_w_o = di("na_w_o", [nl, 512, D])
        self.gm_w_o = di("gm_w_o", [nl, 512, D])
        self.mla_w_o = di("mla_w_o", [nl, 512, D])
        self.w_out = di("w_out", [nl, D, D])
        self.ffn_w1 = di("ffn_w1", [nl, D, 4096])
        self.ffn_w2 = di("ffn_w2", [nl, 4096, D])
        self.out = nc.dram_tensor("out", [NT, D], F32, kind="ExternalOutput").ap()
        dbg = set(debug) if debug else set()
        ds = lambda name, shape, dt: nc.dram_tensor(
            name, list(shape), dt, kind=("ExternalOutput" if name in dbg else "Internal")).ap()
        self.xmid = ds("xmid", [NT, D], F32)
        self.xs = ds("xs", [NT, D], F32)
        self.hT_s = ds("hT_s", [128, 8, NT], BF16)
        self.h2T_s = ds("h2T_s", [128, 8, NT], BF16)
        self.qna = ds("qna", [512, NT], BF16)
        self.kna = ds("kna", [512, NT], BF16)
        self.vna = ds("vna", [NT, 8, 128], BF16)
        self.qml = ds("qml", [96, 8, NT], BF16)
        self.kml = ds("kml", [96, 8, NT], BF16)
        self.vml = ds("vml", [NT, 8, 128], BF16)
        self.ygm = ds("ygm", [512, NT], BF16)
        self.yna = ds("yna", [512, NT], BF16)
        self.ymla = ds("ymla", [512, NT], BF16)
        self.gts = ds("gts", [2, 2, 128, D], F32)
        self.dram = Buf("dram_in")
        self.dscr = Buf("dram_scr")

    def q(self):
        self.qi ^= 1
        return "sp" if self.qi else "act"

    def sb(self, es, name, shape, dt=F32, n=1):
        self.uid = getattr(self, "uid", 0) + 1
        name = "%s_u%d" % (name, self.uid)
        ts = [es.enter_context(self.nc.sbuf_tensor("%s_%d" % (name, i), list(shape), dt)) for i in range(n)]
        return Ring(ts)

    def ps(self, es, name, shape, dt=F32, n=1):
        self.uid = getattr(self, "uid", 0) + 1
        name = "%s_u%d" % (name, self.uid)
        ts = [es.enter_context(self.nc.psum_tensor("%s_%d" % (name, i), list(shape), dt)) for i in range(n)]
        return Ring(ts)

    def act(self, out, in_, func, reads, writes, **kw):
        return self.P.op("act", lambda e: e.activation(out=out, in_=in_, func=func, **kw), reads, writes)

    def tt(self, out, a, b, op, reads, writes, eng="dve"):
        return self.P.op(eng, lambda e: e.tensor_tensor(out, a, b, op), reads, writes)

    def ts(self, out, a, s1, s2, op0, op1, reads, writes, eng="dve"):
        if s2 is None:
            return self.P.op(eng, lambda e: e.tensor_scalar(out, a, s1, None, op0), reads, writes)
        return self.P.op(eng, lambda e: e.tensor_scalar(out, a, s1, s2, op0, op1), reads, writes)

    def stt(self, out, a, s, b, op0, op1, reads, writes, eng="dve"):
        return self.P.op(eng, lambda e: e.scalar_tensor_tensor(out, a, s, b, op0, op1), reads, writes)

    def mm(self, out, lhsT, rhs, start, stop, reads, writes, signal=True):
        return self.P.op("pe", lambda e: e.matmul(out, lhsT, rhs, start=start, stop=stop), reads, writes, signal=signal)

    def tr(self, out, in_, ident, reads, writes, signal=True):
        return self.P.op("pe", lambda e: e.transpose(out, in_, ident), reads, writes, signal=signal)

    def rsqrt(self, r, rb, ss, ssb, scale):
        self.act(r, ss, AF.Sqrt, [ssb], [rb], bias=self.epsc[:, 0:1], scale=scale)
        self.P.op("dve", lambda e: e.reciprocal(r, r), [rb], [rb])

    def load_w(self, es_unused, dst, dstb, src, kch, ncols, stage):
        srcv = src.rearrange("(k p) n -> p k n", p=128)
        engs = ["dve", "pool", "act"]
        i = 0
        for k0 in range(0, kch, 8):
            kk = min(8, kch - k0)
            for c0 in range(0, ncols, 512):
                cw = min(512, ncols - c0)
                st, stb = stage.next()
                self.P.dma(self.q(), st[:, 0:kk, 0:cw], srcv[:, k0:k0 + kk, c0:c0 + cw], [], [stb], stb)
                e = engs[i % 3]
                i += 1
                o = dst[:, k0:k0 + kk, c0:c0 + cw]
                s = st[:, 0:kk, 0:cw]
                if e == "act":
                    self.act(o, s, AF.Copy, [stb], [dstb])
                else:
                    self.P.op(e, lambda en, o=o, s=s: en.tensor_copy(o, s), [stb], [dstb])

    def setup_consts(self, es):
        nc = self.nc
        self.ident = self.sb(es, "ident", [128, 128], BF16)
        self.identf = self.sb(es, "identf", [128, 128], F32)
        self.epsc = es.enter_context(nc.sbuf_tensor("epsc", [128, 1], F32))
        self.onesf = self.sb(es, "onesf", [1, 128], F32)
        idf, idfb = self.identf.t[0], self.identf.b[0]
        idb, idbb = self.ident.t[0], self.ident.b[0]
        self.epsb = Buf()
        self.P.op("pool", lambda e: e.memset(self.epsc[:], EPS), [], [self.epsb])
        self.P.op("pool", lambda e: e.memset(self.onesf.t[0][:], 1.0), [], [self.onesf.b[0]])
        self.P.op("pool", lambda e: e.memset(idf[:], 0.0), [], [idfb])
        self.P.op("pool", lambda e: e.affine_select(idf[:], idf[:], [[-1, 128]], ALU.not_equal, 1.0, base=0,
                                                     channel_multiplier=1), [idfb], [idfb])
        self.P.op("dve", lambda e: e.tensor_copy(idb[:], idf[:]), [idfb], [idbb])
        self.S = self.sb(es, "S", [128, 8, 2], F32)
        self.Sbc = self.sb(es, "Sbc", [128, 8, 2, 128], F32)
        S, Sb = self.S.t[0], self.S.b[0]
        self.P.dma("sp", S[:], self.cc, [], [Sb], Sb)
        self.act(S[:], S[:], AF.Silu, [Sb], [Sb])
        Sbc, Sbcb = self.Sbc.t[0], self.Sbc.b[0]
        self.P.op("dve", lambda e: e.tensor_copy(Sbc[:], S[:].unsqueeze(3).to_broadcast([128, 8, 2, 128])), [Sb], [Sbcb])
        self.modv = self.sb(es, "modv", [128, 4, 8, 2], F32)

    def phase_mod(self, l):
        P = self.P
        with ExitStack() as es:
            stage = self.sb(es, "wmst", [128, 8, 512], F32, 2)
            fm = self.sb(es, "fm", [128, 64], F32)
            bmr = self.sb(es, "bmr", [1, 6144], F32)
            modF = self.sb(es, "modF", [128, 48, 2], F32)
            gt = self.sb(es, "gt", [128, 512], F32, 2)
            pA = self.ps(es, "pA", [128, 48, 2], F32)
            pG = self.ps(es, "pG", [128, 512], F32, 2)
            fmt, fmb = fm.t[0], fm.b[0]
            P.dma("sp", fmt[:], self.fm[l], [], [fmb], fmb)
            bmt, bmb = bmr.t[0], bmr.b[0]
            P.dma("act", bmt[:], self.bmrow[l], [], [bmb], bmb)
            S, Sb = self.S.t[0], self.S.b[0]
            Sbc, Sbcb = self.Sbc.t[0], self.Sbc.b[0]
            pAt, pAb = pA.t[0], pA.b[0]
            wv = self.w_mod[l].rearrange("(k p) n -> p k n", p=128)
            for blk in range(12):
                st, stb = stage.next()
                P.dma(self.q(), st[:], wv[:, :, blk * 512:(blk + 1) * 512], [], [stb], stb)
                for jj in range(4):
                    j = blk * 4 + jj
                    for k in range(8):
                        self.mm(pAt[:, j, :], st[:, k, jj * 128:(jj + 1) * 128], S[:, k, :], k == 0, k == 7,
                                [stb, Sb], [pAb], signal=(k == 7 and jj == 3))
                which = {4: (0, 0), 5: (0, 1), 10: (1, 0), 11: (1, 1)}.get(blk)
                if which is not None:
                    for s in range(2):
                        pg, pgb = pG.next()
                        for k in range(8):
                            self.mm(pg[:], Sbc[:, k, s, :], st[:, k, :], k == 0, False, [Sbcb, stb], [pgb], signal=False)
                        self.mm(pg[:], self.onesf.t[0][0:1, :], bmt[0:1, blk * 512:(blk + 1) * 512], False, True,
                                [self.onesf.b[0], bmb], [pgb])
                        g, gb = gt.next()
                        self.act(g[:], pg[:], AF.Copy, [pgb], [gb])
                        P.dma(self.q(), self.gts[which[0], s, :, which[1] * 512:(which[1] + 1) * 512], g[:], [gb],
                              [], gb)
            mF, mFb = modF.t[0], modF.b[0]
            self.tt(mF[:], pAt[:], fmt[:, 0:48].unsqueeze(2).to_broadcast([128, 48, 2]), ALU.add, [pAb, fmb], [mFb])
            mv, mvb = self.modv.t[0], self.modv.b[0]
            g1 = fmt[:, 48:56].unsqueeze(2).to_broadcast([128, 8, 2])
            g2 = fmt[:, 56:64].unsqueeze(2).to_broadcast([128, 8, 2])
            self.stt(mv[:, 0], mF[:, 8:16, :], 1.0, g1, ALU.add, ALU.mult, [mFb, fmb], [mvb])
            self.P.op("dve", lambda e: e.tensor_copy(mv[:, 1], mF[:, 0:8, :]), [mFb], [mvb])
            self.stt(mv[:, 2], mF[:, 32:40, :], 1.0, g2, ALU.add, ALU.mult, [mFb, fmb], [mvb])
            self.P.op("dve", lambda e: e.tensor_copy(mv[:, 3], mF[:, 24:32, :]), [mFb], [mvb])
            P.barrier()

    def norm_mod_T(self, xt, xb, which, s, R, hT, hTb):
        junk, junkb = R["junk"].next()
        ss, ssb = R["ss"].next()
        self.act(junk[:], xt[:], AF.Square, [xb], [junkb, ssb], accum_out=ss[:, 0:1])
        r, rb = R["r"].next()
        self.rsqrt(r[:, 0:1], rb, ss[:, 0:1], ssb, 1.0 / D)
        xn, xnb = R["xn"].next()
        self.ts(xn[:], xt[:], r[:, 0:1], None, ALU.mult, None, [xb, rb], [xnb])
        ptr, ptrb = R["pb"].next()
        idb, idbb = self.ident.t[0], self.ident.b[0]
        for c in range(8):
            self.tr(ptr[:, c * 128:(c + 1) * 128], xn[:, c * 128:(c + 1) * 128], idb[:], [xnb, idbb], [ptrb],
                    signal=(c == 7))
        mv, mvb = self.modv.t[0], self.modv.b[0]
        for c in range(8):
            self.act(hT[:, c, :], ptr[:, c * 128:(c + 1) * 128], AF.Identity, [ptrb, mvb], [hTb],
                     scale=mv[:, 2 * which, c, s:s + 1], bias=mv[:, 2 * which + 1, c, s:s + 1])

    def phase_inproj(self, l, xsrc):
        P = self.P
        with ExitStack() as es:
            stage = self.sb(es, "st1", [128, 8, 512], F32, 2)
            win = self.sb(es, "win", [128, 8, NTOK_COLS], BF16)
            wuq = self.sb(es, "wuq", [128, 2, 768], BF16)
            wukv = self.sb(es, "wukv", [128, 1, 1024], BF16)
            wst = self.sb(es, "wst", [128, 512], BF16)
            wstf = self.sb(es, "wstf", [128, 512], F32)
            vec = self.sb(es, "vec", [128, NVEC], F32)
            R = {
                "junk": self.sb(es, "junk", [128, 1024], BF16, 2),
                "ss": self.sb(es, "ss", [128, 1], F32, 4),
                "r": self.sb(es, "r", [128, 1], F32, 4),
                "xn": self.sb(es, "xn", [128, 1024], BF16, 2),
                "pb": self.ps(es, "pb", [128, 1024], BF16, 2),
            }
            pf = self.ps(es, "pf", [128, 1024], F32, 3)
            xt_r = self.sb(es, "xt", [128, 1024], F32, 2)
            hT_r = self.sb(es, "hT", [128, 8, 128], BF16, 2)
            sq_r = self.sb(es, "sq", [128, 1024], F32, 2)
            f1_r = self.sb(es, "f1", [128, 1024], F32, 2)
            f2_r = self.sb(es, "f2", [128, 1024], F32, 2)
            b1_r = self.sb(es, "b1", [128, 1024], BF16, 3)
            st8_r = self.sb(es, "st8", [128, 8], F32, 6)
            s1_r = self.sb(es, "s1", [128, 1], F32, 12)
            tb_r = self.sb(es, "tb", [128, 1024], BF16, 3)
            va_r = self.sb(es, "va", [128, 8, 128], BF16, 2)
            vm_r = self.sb(es, "vm", [128, 8, 128], BF16, 2)
            cs_r = self.sb(es, "cs", [128, 64], F32, 2)
            sm_r = self.sb(es, "sm", [128, 32], F32, 8)
            ct_r = self.sb(es, "ct", [128, 2, 128], BF16, 2)
            for t_, b_ in zip(va_r.t + vm_r.t, va_r.b + vm_r.b):
                P.op("pool", lambda e, t_=t_: e.memset(t_[:], 1.0), [], [b_])
            wint, winb = win.t[0], win.b[0]
            self.load_w(es, wint, winb, self.w_in[l][:, 0:NTOK_COLS], 8, NTOK_COLS, stage)
            self.load_w(es, wuq.t[0], wuq.b[0], self.w_uq[l], 2, 768, stage)
            self.load_w(es, wukv.t[0], wukv.b[0], self.w_ukv[l], 1, 1024, stage)
            P.dma("sp", wstf.t[0][:], self.wsT[l], [], [wstf.b[0]], wstf.b[0])
            P.op("dve", lambda e: e.tensor_copy(wst.t[0][:], wstf.t[0][:]), [wstf.b[0]], [wst.b[0]])
            vt, vb = vec.t[0], vec.b[0]
            P.dma("act", vt[:], self.vecs[l], [], [vb], vb)
            idb, idbb = self.ident.t[0], self.ident.b[0]

            def bc_h(v0, n, nh):
                return vt[:, v0:v0 + n].unsqueeze(1).to_broadcast([128, nh, n])

            def headnorm(ps_ap, psb, nh, hd, v_gain):
                sq, sqb = sq_r.next()
                n = nh * hd
                self.act(sq[:, 0:n], ps_ap, AF.Square, [psb], [sqb])
                ssh, sshb = st8_r.next()
                P.op("dve", lambda e: e.tensor_reduce(ssh[:, 0:nh], sq[:, 0:n].rearrange("p (h d) -> p h d", h=nh),
                                                      AX.X, ALU.add), [sqb], [sshb])
                rh, rhb = st8_r.next()
                self.rsqrt(rh[:, 0:nh], rhb, ssh[:, 0:nh], sshb, 1.0 / hd)
                f1, f1b = f1_r.next()
                f13 = f1[:, 0:n].rearrange("p (h d) -> p h d", h=nh)
                self.tt(f13, ps_ap.rearrange("p (h d) -> p h d", h=nh),
                        rh[:, 0:nh].unsqueeze(2).to_broadcast([128, nh, hd]), ALU.mult, [psb, rhb], [f1b])
                return f1, f1b, f13

            def issue_load(t):
                tk = slice(t * 128, (t + 1) * 128)
                xt, xb = xt_r.next()
                P.dma("sp", xt[:], xsrc[tk, :], [], [xb], xb)
                cs, csb = cs_r.next()
                P.dma("sp", cs[:], self.ropet[tk, :], [], [csb], csb)
                return xt, xb, cs, csb

            nxt = issue_load(0)
            for t in range(NTILE):
                s = 1 if t < 2 else 0
                tok = slice(t * 128, (t + 1) * 128)
                xt, xb, cs, csb = nxt
                if t + 1 < NTILE:
                    nxt = issue_load(t + 1)
                hT, hTb = hT_r.next()
                self.norm_mod_T(xt, xb, 0, s, R, hT, hTb)
                P.dma("act", self.hT_s[:, :, tok], hT[:], [hTb], [], hTb)

                def proj(c0, ncols, pst, psb):
                    for h0 in range(0, ncols, 512):
                        hw = min(512, ncols - h0)
                        for k in range(8):
                            self.mm(pst[:, h0:h0 + hw], hT[:, k, :], wint[:, k, c0 + h0:c0 + h0 + hw], k == 0, k == 7,
                                    [hTb, winb], [psb], signal=(k == 7))

                for (c0, vg, dst) in ((OFF_NA_K, V_GKNA, self.kna), (OFF_NA_Q, V_GQNA, self.qna)):
                    pst, psb = pf.next()
                    proj(c0, 512, pst, psb)
                    f1, f1b, f13 = headnorm(pst[:, 0:512], psb, 8, 64, vg)
                    b1, b1b = b1_r.next()
                    self.tt(b1[:, 0:512].rearrange("p (h d) -> p h d", h=8), f13, bc_h(vg, 64, 8), ALU.mult,
                            [f1b, vb], [b1b])
                    ptr, ptrb = R["pb"].next()
                    for c in range(4):
                        self.tr(ptr[:, c * 128:(c + 1) * 128], b1[:, c * 128:(c + 1) * 128], idb[:], [b1b, idbb],
                                [ptrb], signal=(c == 3))
                    tb, tbb = tb_r.next()
                    self.act(tb[:, 0:512], ptr[:, 0:512], AF.Copy, [ptrb], [tbb])
                    P.dma("act", dst.rearrange("(c p) n -> p c n", p=128)[:, :, tok],
                          tb[:, 0:512].rearrange("p (c n) -> p c n", c=4), [tbb], [], tbb)
                pst, psb = pf.next()
                proj(OFF_NA_V, 512, pst, psb)
                va, vab = va_r.next()
                self.act(va[:, :, 0:64], pst[:, 0:512].rearrange("p (h d) -> p h d", h=8), AF.Copy, [psb], [vab])
                P.dma("act", self.vna[tok, :, :], va[:], [vab], [], vab)
                pst, psb = pf.next()
                proj(OFF_CKV, 160, pst, psb)
                junk, junkb = R["junk"].next()
                ssc, sscb = s1_r.next()
                self.act(junk[:, 0:128], pst[:, 0:128], AF.Square, [psb], [junkb, sscb], accum_out=ssc[:, 0:1])
                ssr, ssrb = s1_r.next()
                self.act(junk[:, 128:160], pst[:, 128:160], AF.Square, [psb], [junkb, ssrb], accum_out=ssr[:, 0:1])
                rc, rcb = s1_r.next()
                self.rsqrt(rc[:, 0:1], rcb, ssc[:, 0:1], sscb, 1.0 / 128)
                b1, b1b = b1_r.next()
                self.stt(b1[:, 0:128], pst[:, 0:128], rc[:, 0:1], vt[:, V_GCKV:V_GCKV + 128], ALU.mult, ALU.mult,
                         [psb, rcb, vb], [b1b])
                krg, krgb = sm_r.next()
                self.tt(krg[:], pst[:, 128:160], vt[:, V_GKM + 64:V_GKM + 96], ALU.mult, [psb, vb], [krgb])
                ptr, ptrb = R["pb"].next()
                self.tr(ptr[:, 0:128], b1[:, 0:128], idb[:], [b1b, idbb], [ptrb])
                ct, ctb = ct_r.next()
                self.act(ct[:, 0, :], ptr[:, 0:128], AF.Copy, [ptrb], [ctb])
                pkv, pkvb = pf.next()
                wk, wkb = wukv.t[0], wukv.b[0]
                self.mm(pkv[:, 0:512], ct[:, 0, :], wk[:, 0, 0:512], True, True, [ctb, wkb], [pkvb], signal=False)
                self.mm(pkv[:, 512:1024], ct[:, 0, :], wk[:, 0, 512:1024], True, True, [ctb, wkb], [pkvb])
                kv3 = pkv[:].rearrange("p (h d) -> p h d", h=8)
                vm, vmb = vm_r.next()
                self.act(vm[:, :, 0:64], kv3[:, :, 64:128], AF.Copy, [pkvb], [vmb])
                P.dma("act", self.vml[tok, :, :], vm[:], [vmb], [], vmb)
                sq, sqb = sq_r.next()
                self.act(sq[:], pkv[:], AF.Square, [pkvb], [sqb])
                ssn, ssnb = st8_r.next()
                P.op("dve", lambda e: e.tensor_reduce(ssn[:], sq[:].rearrange("p (h d) -> p h d", h=8)[:, :, 0:64],
                                                      AX.X, ALU.add), [sqb], [ssnb])
                self.ts(ssn[:], ssn[:], ssr[:, 0:1], None, ALU.add, None, [ssnb, ssrb], [ssnb])
                rkm, rkmb = st8_r.next()
                self.rsqrt(rkm[:], rkmb, ssn[:], ssnb, 1.0 / 96)
                f1, f1b = f1_r.next()
                f13 = f1[:, 0:512].rearrange("p (h d) -> p h d", h=8)
                self.tt(f13, kv3[:, :, 0:64], rkm[:].unsqueeze(2).to_broadcast([128, 8, 64]), ALU.mult,
                        [pkvb, rkmb], [f1b])
                kb, kbb = b1_r.next()
                kb3 = kb[:, 0:768].rearrange("p (h d) -> p h d", h=8)
                self.tt(kb3[:, :, 0:64], f13, bc_h(V_GKM, 64, 8), ALU.mult, [f1b, vb], [kbb])
                t1, t1b = sm_r.next()
                self.tt(t1[:], krg[:], cs[:, 0:32], ALU.mult, [krgb, csb], [t1b])
                t2, t2b = sm_r.next()
                k4 = krg[:].rearrange("p (a s j) -> p a s j", a=2, s=2)
                s4 = cs[:, 32:64].rearrange("p (a s j) -> p a s j", a=2, s=2)
                t24 = t2[:].rearrange("p (a s j) -> p a s j", a=2, s=2)
                self.tt(t24[:, :, 0, :], k4[:, :, 1, :], s4[:, :, 0, :], ALU.mult, [krgb, csb], [t2b])
                self.tt(t24[:, :, 1, :], k4[:, :, 0, :], s4[:, :, 1, :], ALU.mult, [krgb, csb], [t2b])
                self.tt(t1[:], t1[:], t2[:], ALU.add, [t1b, t2b], [t1b])
                self.tt(kb3[:, :, 64:96], t1[:].unsqueeze(1).to_broadcast([128, 8, 32]),
                        rkm[:].unsqueeze(2).to_broadcast([128, 8, 32]), ALU.mult, [t1b, rkmb], [kbb])
                ptr, ptrb = R["pb"].next()
                for h in range(8):
                    self.tr(ptr[0:96, h * 128:(h + 1) * 128], kb[:, h * 96:(h + 1) * 96], idb[:], [kbb, idbb], [ptrb],
                            signal=(h == 7))
                tb, tbb = tb_r.next()
                self.act(tb[0:96, :], ptr[0:96, :], AF.Copy, [ptrb], [tbb])
                P.dma("act", self.kml[:, :, tok], tb[0:96, :].rearrange("p (h n) -> p h n", h=8), [tbb],
                      [], tbb)
                pst, psb = pf.next()
                proj(OFF_CQ, 256, pst, psb)
                junk, junkb = R["junk"].next()
                ssq, ssqb = s1_r.next()
                self.act(junk[:, 0:256], pst[:, 0:256], AF.Square, [psb], [junkb, ssqb], accum_out=ssq[:, 0:1])
                rq, rqb = s1_r.next()
                self.rsqrt(rq[:, 0:1], rqb, ssq[:, 0:1], ssqb, 1.0 / 256)
                b1, b1b = b1_r.next()
                self.stt(b1[:, 0:256], pst[:, 0:256], rq[:, 0:1], vt[:, V_GCQ:V_GCQ + 256], ALU.mult, ALU.mult,
                         [psb, rqb, vb], [b1b])
                ptr, ptrb = R["pb"].next()
                for c in range(2):
                    self.tr(ptr[:, c * 128:(c + 1) * 128], b1[:, c * 128:(c + 1) * 128], idb[:], [b1b, idbb], [ptrb],
                            signal=(c == 1))
                ct, ctb = ct_r.next()
                self.act(ct[:], ptr[:, 0:256].rearrange("p (c n) -> p c n", c=2), AF.Copy, [ptrb], [ctb])
                pq, pqb = pf.next()
                wq, wqb = wuq.t[0], wuq.b[0]
                for (h0, hw) in ((0, 512), (512, 256)):
                    for k in range(2):
                        self.mm(pq[:, h0:h0 + hw], ct[:, k, :], wq[:, k, h0:h0 + hw], k == 0, k == 1, [ctb, wqb], [pqb],
                                signal=(k == 1 and h0 == 512))
                f1, f1b, f13 = headnorm(pq[:, 0:768], pqb, 8, 96, V_GQM)
                f2, f2b = f2_r.next()
                f23 = f2[:, 0:768].rearrange("p (h d) -> p h d", h=8)
                self.tt(f23, f13, bc_h(V_GQM, 96, 8), ALU.mult, [f1b, vb], [f2b])
                qb_, qbb = b1_r.next()
                qb3 = qb_[:, 0:768].rearrange("p (h d) -> p h d", h=8)
                P.op("pool", lambda e: e.tensor_copy(qb3[:, :, 0:64], f23[:, :, 0:64]), [f2b], [qbb])
                f1, f1b = f1_r.next()
                t13 = f1[:, 0:256].rearrange("p (h d) -> p h d", h=8)
                t23 = f1[:, 256:512].rearrange("p (h d) -> p h d", h=8)
                self.tt(t13, f23[:, :, 64:96], cs[:, 0:32].unsqueeze(1).to_broadcast([128, 8, 32]), ALU.mult,
                        [f2b, csb], [f1b])
                q5 = f2[:, 0:768].rearrange("p (h d) -> p h d", h=8)[:, :, 64:96].rearrange(
                    "p h (a s j) -> p h a s j", a=2, s=2)
                t25 = t23.rearrange("p h (a s j) -> p h a s j", a=2, s=2)
                s5 = cs[:, 32:64].rearrange("p (a s j) -> p a s j", a=2, s=2)
                for sidx in range(2):
                    self.tt(t25[:, :, :, sidx, :], q5[:, :, :, 1 - sidx, :],
                            s5[:, :, sidx, :].unsqueeze(1).to_broadcast([128, 8, 2, 8]), ALU.mult, [f2b, csb], [f1b])
                self.tt(qb3[:, :, 64:96], t13, t23, ALU.add, [f1b], [qbb])
                ptr, ptrb = R["pb"].next()
                for h in range(8):
                    self.tr(ptr[0:96, h * 128:(h + 1) * 128], qb_[:, h * 96:(h + 1) * 96], idb[:], [qbb, idbb], [ptrb],
                            signal=(h == 7))
                tb, tbb = tb_r.next()
                self.act(tb[0:96, :], ptr[0:96, :], AF.Copy, [ptrb], [tbb])
                P.dma("act", self.qml[:, :, tok], tb[0:96, :].rearrange("p (h n) -> p h n", h=8), [tbb],
                      [], tbb)
                pst, psb = pf.next()
                proj(OFF_GM_V, 512, pst, psb)
                f1, f1b = f1_r.next()
                sA, sAb = s1_r.next()
                self.act(f1[:, 0:512], pst[:, 0:512], AF.Gelu, [psb], [f1b, sAb], accum_out=sA[:, 0:1])
                junk, junkb = R["junk"].next()
                sB, sBb = s1_r.next()
                self.act(junk[:, 0:512], f1[:, 0:512], AF.Square, [f1b], [junkb, sBb], accum_out=sB[:, 0:1])
                mean, meanb = s1_r.next()
                self.ts(mean[:, 0:1], sA[:, 0:1], 1.0 / 512, None, ALU.mult, None, [sAb], [meanb])
                m2, m2b = s1_r.next()
                self.tt(m2[:, 0:1], mean[:, 0:1], mean[:, 0:1], ALU.mult, [meanb], [m2b])
                var, varb = s1_r.next()
                self.stt(var[:, 0:1], sB[:, 0:1], 1.0 / 512, m2[:, 0:1], ALU.mult, ALU.subtract, [sBb, m2b], [varb])
                rs, rsb = s1_r.next()
                self.rsqrt(rs[:, 0:1], rsb, var[:, 0:1], varb, 1.0)
                f2, f2b = f2_r.next()
                self.ts(f2[:, 0:512], f1[:, 0:512], mean[:, 0:1], rs[:, 0:1], ALU.subtract, ALU.mult,
                        [f1b, meanb, rsb], [f2b])
                self.tt(f2[:, 0:512], f2[:, 0:512], vt[:, V_LNG:V_LNG + 512], ALU.mult, [f2b, vb], [f2b], eng="pool")
                b1, b1b = b1_r.next()
                self.tt(b1[:, 0:512], f2[:, 0:512], vt[:, V_LNB:V_LNB + 512], ALU.add, [f2b, vb], [b1b], eng="pool")
                pmx, pmxb = pf.next()
                for g in range(4):
                    self.mm(pmx[:, g * 128:(g + 1) * 128], b1[:, g * 128:(g + 1) * 128],
                            wst.t[0][:, g * 128:(g + 1) * 128], True, True, [b1b, wst.b[0]], [pmxb], signal=(g == 3))
                for g in range(4):
                    for k in range(8):
                        self.mm(pmx[:, 512 + g * 128:512 + (g + 1) * 128],
                                wint[:, k, OFF_GM_U + g * 128:OFF_GM_U + (g + 1) * 128], hT[:, k, :], k == 0, k == 7,
                                [hTb, winb], [pmxb], signal=(k == 7 and g == 3))
                f1, f1b = f1_r.next()
                self.act(f1[:, 0:512], pmx[:, 512:1024], AF.Gelu, [pmxb], [f1b])
                f2, f2b = f2_r.next()
                self.tt(f2[:, 0:512], pmx[:, 0:512], vt[:, V_BS:V_BS + 512], ALU.add, [pmxb, vb], [f2b])
                tb, tbb = tb_r.next()
                self.tt(tb[:, 0:512], f1[:, 0:512], f2[:, 0:512], ALU.mult, [f1b, f2b], [tbb], eng="pool")
                P.dma("act", self.ygm.rearrange("(c p) n -> p c n", p=128)[:, :, tok],
                      tb[:, 0:512].rearrange("p (c n) -> p c n", c=4), [tbb], [], tbb)
            P.barrier()


    def finalize(self, po, pob, N, FR, dst_ap, src_view=None):
        P = self.P
        idb, idbb = self.ident.t[0], self.ident.b[0]
        dh, dhb = FR["dh"].next()
        dl, dlb = FR["dl"].next()
        self.act(dh[64:128, 0:N], po[64:128, 0:N], AF.Copy, [pob], [dhb])
        self.tt(dl[64:128, 0:N], po[64:128, 0:N], dh[64:128, 0:N], ALU.subtract, [pob, dhb], [dlb])
        pd, pdb = FR["pd"].next()
        self.mm(pd[0:64, 0:N], idb[64:128, 64:128], dh[64:128, 0:N], True, False, [idbb, dhb], [pdb], signal=False)
        self.mm(pd[0:64, 0:N], idb[64:128, 64:128], dl[64:128, 0:N], False, True, [idbb, dlb], [pdb])
        rd, rdb = FR["rd"].next()
        P.op("dve", lambda e: e.reciprocal(rd[0:64, 0:N], pd[0:64, 0:N]), [pdb], [rdb])
        yt, ytb = FR["yt"].next()
        self.tt(yt[0:64, 0:N], po[0:64, 0:N], rd[0:64, 0:N], ALU.mult, [pob, rdb], [ytb])
        src = yt[0:64, 0:N] if src_view is None else src_view(yt)
        P.dma("act", dst_ap, src, [ytb], [], ytb)

    def fin_rings(self, es):
        return {
            "dh": self.sb(es, "dh", [128, 512], BF16, 2),
            "dl": self.sb(es, "dl", [128, 512], BF16, 2),
            "rd": self.sb(es, "rd", [64, 512], F32, 2),
            "yt": self.sb(es, "yt", [64, 512], BF16, 3),
            "pd": self.ps(es, "pd", [128, 512], F32, 2),
        }

    def phase_na(self, l, last):
        P = self.P
        with ExitStack() as es:
            Tf = self.sb(es, "Tf", [128, 8, 16, 64], F32)
            Ti = self.sb(es, "Ti", [128, 8, 16, 64], F32)
            m01 = self.sb(es, "m01", [128, 2, 1024], F32)
            qg_r = self.sb(es, "qg", [128, 2, 2, NT], BF16)
            kg_r = self.sb(es, "kg", [128, 2, NT], BF16)
            vg_r = self.sb(es, "vg", [128, NTILE, 4, 128], BF16)
            e_r = self.sb(es, "e", [128, 512], F32, 2)
            pT_r = self.sb(es, "pT", [128, 512], BF16, 4)
            ps_r = self.ps(es, "ps", [128, 512], F32, 3)
            po_r = self.ps(es, "po", [128, 512], F32, 2)
            FR = self.fin_rings(es)
            tf, tfb = Tf.t[0], Tf.b[0]
            ti, tib = Ti.t[0], Ti.b[0]
            mt, mb = m01.t[0], m01.b[0]
            for h in range(8):
                P.dma("sp", tf[:, h].rearrange("p u q -> p (u q)"), self.rpbg[l][:, h * 1024:(h + 1) * 1024], [], [tfb], tfb)
            for k in range(2):
                P.dma("sp", mt[:, k, :], self.m01[:, k * 1024:(k + 1) * 1024], [], [mb], mb)
            for h in range(8):
                self.act(tf[:, h], tf[:, h], AF.Exp, [tfb], [tfb])
            for h in range(8):
                th = tf[:, h].rearrange("p u q -> p (u q)")
                self.tt(ti[:, h].rearrange("p u q -> p (u q)"), th, mt[:, 1, :], ALU.mult, [tfb, mb], [tib])
                self.tt(th, th, mt[:, 0, :], ALU.mult, [tfb, mb], [tfb])
            qv = self.qna.rearrange("(c p) n -> p c n", p=128)
            kv = self.kna.rearrange("(c p) n -> p c n", p=128)
            vv = self.vna.rearrange("(t p) h d -> p t h d", p=128)
            yv = self.yna.rearrange("(h d) n -> d h n", d=64)
            for g in range(2):
                qg, qgb = qg_r.next()
                kg, kgb = kg_r.next()
                vg, vgb = vg_r.next()
                if g == 0:
                    P.op("pool", lambda e: e.memset(qg[:], 0.0), [], [qgb])
                for hp in range(2):
                    P.dma("sp", qg[hp * 64:(hp + 1) * 64, :, hp, :], qv[hp * 64:(hp + 1) * 64, 2 * g:2 * g + 2, :], [],
                          [qgb], qgb)
                P.dma("sp", kg[:], kv[:, 2 * g:2 * g + 2, :], [], [kgb], kgb)
                for t0 in range(0, NTILE, 6):
                    t1 = min(NTILE, t0 + 6)
                    P.dma("sp", vg[:, t0:t1], vv[:, t0:t1, 4 * g:4 * g + 4, :], [], [vgb], vgb)
                units = []
                if not last:
                    for qt in range(2):
                        units.append((qt * 128, [(0, None, 0), (1, None, 0)]))
                for m in range(32):
                    kts = []
                    if 2 <= m <= 29:
                        for i in range(5):
                            kts.append((m + i, ti, 7 - (-4 + 2 * i)))
                    elif m < 2:
                        for i in range(4):
                            kts.append((2 + i, tf, 7 - (2 * i - 2 * m)))
                    else:
                        for i in range(4):
                            kts.append((30 + i, tf, 7 - (56 + 2 * i - 2 * m)))
                    kts += [(0, None, 0), (1, None, 0)]
                    units.append((CL + m * 128, kts))
                for (q0, kts) in units:
                    po, pob = po_r.next()
                    nk = len(kts)
                    for i, (j, tab, u0) in enumerate(kts):
                        ps, psb = ps_r.next()
                        for hh in range(4):
                            c, hp = hh // 2, hh % 2
                            self.mm(ps[:, hh * 128:(hh + 1) * 128], kg[:, c, j * 128:(j + 1) * 128],
                                    qg[:, c, hp, q0:q0 + 128], True, True, [kgb, qgb], [psb], signal=(hh == 3))
                        pT, pTb = pT_r.next()
                        if tab is None:
                            self.act(pT[:], ps[:], AF.Exp, [psb], [pTb], scale=0.125)
                        else:
                            e, eb = e_r.next()
                            self.act(e[:], ps[:], AF.Exp, [psb], [eb], scale=0.125)
                            tbuf = tib if tab is ti else tfb
                            self.tt(pT[:].rearrange("p (h b q) -> p h b q", h=4, b=2),
                                    e[:].rearrange("p (h b q) -> p h b q", h=4, b=2),
                                    tab[:, 4 * g:4 * g + 4, u0:u0 + 2, :], ALU.mult, [eb, tbuf], [pTb])
                        for hh in range(4):
                            self.mm(po[:, hh * 128:(hh + 1) * 128], vg[:, j, hh, :], pT[:, hh * 128:(hh + 1) * 128],
                                    i == 0 and hh == 0, i == nk - 1 and hh == 3, [vgb, pTb], [pob], signal=(hh == 3))
                    self.finalize(po, pob, 512, FR, yv[:, 4 * g:4 * g + 4, q0:q0 + 128],
                                  src_view=lambda yt: yt[0:64, 0:512].rearrange("p (h n) -> p h n", h=4))
            P.barrier()

    def phase_mla(self, l, last):
        P = self.P
        with ExitStack() as es:
            qh_r = self.sb(es, "qh", [96, NT], BF16, 2)
            kh_r = self.sb(es, "kh", [96, NT], BF16, 2)
            vh_r = self.sb(es, "vh", [128, NTILE, 128], BF16, 2)
            pT_r = self.sb(es, "pTm", [128, 512], BF16, 4)
            ps_r = self.ps(es, "psm", [128, 512], F32, 3)
            po_r = self.ps(es, "pom", [128, 512], F32, 2)
            FR = self.fin_rings(es)
            vv = self.vml.rearrange("(t p) h d -> p t h d", p=128)
            sc = 96.0 ** -0.5

            def load(h):
                qh, qhb = qh_r.next()
                kh, khb = kh_r.next()
                vh, vhb = vh_r.next()
                P.dma("sp", qh[:], self.qml[:, h, :], [], [qhb], qhb)
                P.dma("sp", kh[:], self.kml[:, h, :], [], [khb], khb)
                for t0 in range(0, NTILE, 6):
                    t1 = min(NTILE, t0 + 6)
                    P.dma("sp", vh[:, t0:t1], vv[:, t0:t1, h, :], [], [vhb], vhb)
                return qh, qhb, kh, khb, vh, vhb

            nxt = load(0)
            for h in range(8):
                qh, qhb, kh, khb, vh, vhb = nxt
                if h + 1 < 8:
                    nxt = load(h + 1)
                units = []
                if not last:
                    units.append((0, 256, [0, 1]))
                for qb in range(8):
                    units.append((CL + qb * 512, 512, list(range(NTILE))))
                for (q0, N, kts) in units:
                    po, pob = po_r.next()
                    nk = len(kts)
                    for i, j in enumerate(kts):
                        ps, psb = ps_r.next()
                        self.mm(ps[:, 0:N], kh[:, j * 128:(j + 1) * 128], qh[:, q0:q0 + N], True, True, [khb, qhb], [psb])
                        pT, pTb = pT_r.next()
                        self.act(pT[:, 0:N], ps[:, 0:N], AF.Exp, [psb], [pTb], scale=sc)
                        self.mm(po[:, 0:N], vh[:, j, :], pT[:, 0:N], i == 0, i == nk - 1, [vhb, pTb], [pob],
                                signal=(i == nk - 1))
                    self.finalize(po, pob, N, FR, self.ymla[h * 64:(h + 1) * 64, q0:q0 + N])
            P.barrier()

    def phase_merge(self, l, last, xsrc):
        P = self.P
        NB = 256
        with ExitStack() as es:
            stage = self.sb(es, "st4", [128, 8, 512], F32, 2)
            wg = self.sb(es, "wg", [128, 8, 3072], BF16)
            wo3 = self.sb(es, "wo3", [128, 3, 4, 1024], BF16)
            wout = self.sb(es, "wout", [128, 8, 1024], BF16)
            gt1 = self.sb(es, "gt1", [128, 2, 1024], F32)
            hb_r = self.sb(es, "hb", [128, 8, NB], BF16, 2)
            yb_r = self.sb(es, "yb", [128, 3, 4, NB], BF16, 2)
            ym_r = self.sb(es, "ym", [128, 8, NB], BF16, 1)
            g_r = self.sb(es, "g4", [128, NB], F32, 3)
            t_r = self.sb(es, "t4", [128, NB], F32, 3)
            acc_r = self.sb(es, "acc4", [128, NB], F32, 2)
            xt_r = self.sb(es, "xt4", [128, 1024], F32, 2)
            xo_r = self.sb(es, "xo4", [128, 1024], F32, 2)
            tm_r = self.sb(es, "tm4", [128, 512], F32, 2)
            h2_r = self.sb(es, "h2", [128, 8, 128], BF16, 2)
            R = {
                "junk": self.sb(es, "junk4", [128, 1024], BF16, 1),
                "ss": self.sb(es, "ss4", [128, 1], F32, 4),
                "r": self.sb(es, "r4", [128, 1], F32, 4),
                "xn": self.sb(es, "xn4", [128, 1024], BF16, 2),
                "pb": self.ps(es, "pb4", [128, 1024], BF16, 1),
            }
            ps_r = self.ps(es, "ps4", [128, 512], F32, 6)
            self.load_w(es, wg.t[0], wg.b[0], self.w_in[l][:, OFF_GATES:N_IN], 8, 3072, stage)
            for br, wsrc in enumerate((self.na_w_o, self.gm_w_o, self.mla_w_o)):
                self.load_w(es, wo3.t[0][:, br], wo3.b[0], wsrc[l], 4, 1024, stage)
            self.load_w(es, wout.t[0], wout.b[0], self.w_out[l], 8, 1024, stage)
            g1t, g1b = gt1.t[0], gt1.b[0]
            P.dma("sp", g1t[:], self.gts[0].rearrange("s p n -> p s n"), [], [g1b], g1b)
            wgt, wgb = wg.t[0], wg.b[0]
            wot, wob = wo3.t[0], wo3.b[0]
            wut, wub = wout.t[0], wout.b[0]
            ysrcs = [d.rearrange("(c p) n -> p c n", p=128) for d in (self.yna, self.ygm, self.ymla)]
            blocks = list(range(1 if last else 0, NT // NB))

            def load(blk):
                t0 = blk * NB
                hb, hbb = hb_r.next()
                yb, ybb = yb_r.next()
                P.dma("sp", hb[:], self.hT_s[:, :, t0:t0 + NB], [], [hbb], hbb)
                for br in range(3):
                    P.dma("sp", yb[:, br], ysrcs[br][:, :, t0:t0 + NB], [], [ybb], ybb)
                return hb, hbb, yb, ybb

            nxt = load(blocks[0])
            for bi, blk in enumerate(blocks):
                s = 1 if blk == 0 else 0
                hb, hbb, yb, ybb = nxt
                if bi + 1 < len(blocks):
                    nxt = load(blocks[bi + 1])
                xts = []
                for tt in range(2):
                    xt, xb = xt_r.next()
                    tk = slice(blk * NB + tt * 128, blk * NB + (tt + 1) * 128)
                    P.dma("sp", xt[:], xsrc[tk, :], [], [xb], xb)
                    xts.append((xt, xb, tk))
                ym, ymb = ym_r.next()
                for f in range(8):
                    acc, accb = acc_r.next()
                    for br in range(3):
                        pg, pgb = ps_r.next()
                        for k in range(8):
                            self.mm(pg[:, 0:NB], wgt[:, k, br * 1024 + f * 128:br * 1024 + (f + 1) * 128], hb[:, k, :],
                                    k == 0, k == 7, [wgb, hbb], [pgb], signal=(k == 7))
                        gg, ggb = g_r.next()
                        self.act(gg[:], pg[:, 0:NB], AF.Sigmoid, [pgb], [ggb])
                        pb, pbb = ps_r.next()
                        for k in range(4):
                            self.mm(pb[:, 0:NB], wot[:, br, k, f * 128:(f + 1) * 128], yb[:, br, k, :], k == 0, k == 3,
                                    [wob, ybb], [pbb], signal=(k == 3))
                        if br == 0:
                            self.tt(acc[:], gg[:], pb[:, 0:NB], ALU.mult, [ggb, pbb], [accb])
                        else:
                            tq, tqb = t_r.next()
                            self.tt(tq[:], gg[:], pb[:, 0:NB], ALU.mult, [ggb, pbb], [tqb])
                            if br == 1:
                                self.tt(acc[:], acc[:], tq[:], ALU.add, [accb, tqb], [accb], eng="pool")
                            else:
                                self.tt(ym[:, f, :], acc[:], tq[:], ALU.add, [accb, tqb], [ymb], eng="pool")
                for tt in range(2):
                    xt, xb, tk = xts[tt]
                    xo, xob = xo_r.next()
                    for nb in range(2):
                        cs_ = slice(nb * 512, (nb + 1) * 512)
                        pm, pmb = ps_r.next()
                        for f in range(8):
                            self.mm(pm[:], ym[:, f, tt * 128:(tt + 1) * 128], wut[:, f, cs_], f == 0, f == 7,
                                    [ymb, wub], [pmb], signal=(f == 7))
                        tm, tmb = tm_r.next()
                        self.tt(tm[:], pm[:], g1t[:, s, cs_], ALU.mult, [pmb, g1b], [tmb])
                        self.tt(xo[:, cs_], tm[:], xt[:, cs_], ALU.add, [tmb, xb], [xob], eng="pool")
                    P.dma("act", self.xmid[tk, :], xo[:], [xob], [], xob)
                    h2, h2b = h2_r.next()
                    self.norm_mod_T(xo, xob, 1, s, R, h2, h2b)
                    P.dma("act", self.h2T_s[:, :, tk], h2[:], [h2b], [], h2b)
            P.barrier()

    def phase_ffn(self, l, last, dst):
        P = self.P
        NB = 256
        with ExitStack() as es:
            stage = self.sb(es, "st5", [128, 4, 512], F32, 2)
            w1 = self.sb(es, "w1", [128, 8, 4096], BF16)
            w2 = self.sb(es, "w2", [128, 32, 1024], BF16)
            gt2 = self.sb(es, "gt2", [128, 2, 1024], F32)
            hb_r = self.sb(es, "hb5", [128, 8, NB], BF16, 2)
            uT_r = self.sb(es, "uT", [128, 32, NB], BF16, 1)
            r_r = self.sb(es, "r5", [128, NB], F32, 3)
            xm_r = self.sb(es, "xm5", [128, 1024], F32, 2)
            tm_r = self.sb(es, "tm5", [128, 512], F32, 2)
            ps_r = self.ps(es, "ps5", [128, 512], F32, 6)
            w1t, w1b = w1.t[0], w1.b[0]
            w2t, w2b = w2.t[0], w2.b[0]
            self.load_w4(w1t, w1b, self.ffn_w1[l], 8, 4096, stage)
            self.load_w4(w2t, w2b, self.ffn_w2[l], 32, 1024, stage)
            g2t, g2b = gt2.t[0], gt2.b[0]
            P.dma("sp", g2t[:], self.gts[1].rearrange("s p n -> p s n"), [], [g2b], g2b)
            blocks = list(range(1 if last else 0, NT // NB))

            def load(blk):
                t0 = blk * NB
                hb, hbb = hb_r.next()
                P.dma("sp", hb[:], self.h2T_s[:, :, t0:t0 + NB], [], [hbb], hbb)
                return hb, hbb

            nxt = load(blocks[0])
            for bi, blk in enumerate(blocks):
                s = 1 if blk == 0 else 0
                hb, hbb = nxt
                if bi + 1 < len(blocks):
                    nxt = load(blocks[bi + 1])
                xms = []
                for tt in range(2):
                    xm, xmb = xm_r.next()
                    tk = slice(blk * NB + tt * 128, blk * NB + (tt + 1) * 128)
                    P.dma("sp", xm[:], self.xmid[tk, :], [], [xmb], xmb)
                    xms.append((xm, xmb, tk))
                uT, uTb = uT_r.next()
                for hc in range(32):
                    pu, pub = ps_r.next()
                    for k in range(8):
                        self.mm(pu[:, 0:NB], w1t[:, k, hc * 128:(hc + 1) * 128], hb[:, k, :], k == 0, k == 7,
                                [w1b, hbb], [pub], signal=(k == 7))
                    rr, rrb = r_r.next()
                    self.act(rr[:], pu[:, 0:NB], AF.Relu, [pub], [rrb])
                    self.tt(uT[:, hc, :], rr[:], rr[:], ALU.mult, [rrb], [uTb], eng=("dve" if hc % 2 == 0 else "pool"))
                for tt in range(2):
                    xm, xmb, tk = xms[tt]
                    for nb in range(2):
                        cs_ = slice(nb * 512, (nb + 1) * 512)
                        py, pyb = ps_r.next()
                        for hc in range(32):
                            self.mm(py[:], uT[:, hc, tt * 128:(tt + 1) * 128], w2t[:, hc, cs_], hc == 0, hc == 31,
                                    [uTb, w2b], [pyb], signal=(hc == 31))
                        tm, tmb = tm_r.next()
                        self.tt(tm[:], py[:], g2t[:, s, cs_], ALU.mult, [pyb, g2b], [tmb])
                        self.tt(xm[:, cs_], tm[:], xm[:, cs_], ALU.add, [tmb, xmb], [xmb], eng="pool")
                    P.dma("act", dst[tk, :], xm[:], [xmb], [], xmb)
            P.barrier()

    def load_w4(self, dst, dstb, src, kch, ncols, stage):
        srcv = src.rearrange("(k p) n -> p k n", p=128)
        engs = ["dve", "pool", "act"]
        i = 0
        for k0 in range(0, kch, 4):
            for c0 in range(0, ncols, 512):
                st, stb = stage.next()
                self.P.dma(self.q(), st[:], srcv[:, k0:k0 + 4, c0:c0 + 512], [], [stb], stb)
                e = engs[i % 3]
                i += 1
                o = dst[:, k0:k0 + 4, c0:c0 + 512]
                if e == "act":
                    self.act(o, st[:], AF.Copy, [stb], [dstb])
                else:
                    self.P.op(e, lambda en, o=o, st=st: en.tensor_copy(o, st[:]), [stb], [dstb])


ALL_PHASES = ("mod", "inproj", "na", "mla", "merge", "ffn")


def build(nl, last_flags, debug=False, phases=ALL_PHASES):
    K = Kern(nl, last_flags, debug)
    with ExitStack() as es:
        K.setup_consts(es)
        for l in range(nl):
            xsrc = K.xin if l == 0 else K.xs
            last = last_flags[l]
            if "mod" in phases:
                K.phase_mod(l)
            if "inproj" in phases:
                K.phase_inproj(l, xsrc)
            if "na" in phases:
                K.phase_na(l, last)
            if "mla" in phases:
                K.phase_mla(l, last)
            if "merge" in phases:
                K.phase_merge(l, last, xsrc)
            if "ffn" in phases:
                K.phase_ffn(l, last, K.out if l == nl - 1 else K.xs)
        K.P.finish()
    print("nins", K.P.nins, "nwait", K.P.nwait, "ndsem", len(K.P.all_ds))
    return K.nc


def _rope_table():
    n = np.arange(SEQ)
    rows = (n // GRID_W).astype(np.float32)
    cols = (n % GRID_W).astype(np.float32)
    freqs = (np.float32(10000.0) ** (-np.arange(8, dtype=np.float32) / np.float32(8))).astype(np.float32)
    ang = np.stack([rows[:, None] * freqs, cols[:, None] * freqs], axis=1).astype(np.float32)
    c, s = np.cos(ang).astype(np.float32), np.sin(ang).astype(np.float32)
    tab = np.zeros((NT, 64), np.float32)
    tab[:CL, 0:32] = 1.0
    cos_t = np.stack([c, c], axis=2).reshape(SEQ, 32)
    sin_t = np.stack([-s, s], axis=2).reshape(SEQ, 32)
    tab[CL:, 0:32] = cos_t
    tab[CL:, 32:64] = sin_t
    return tab


def _na_tables(rpb):
    L = rpb.shape[0]
    a = np.arange(2)[:, None, None, None]
    kc = np.arange(64)[None, :, None, None]
    u = np.arange(16)[None, None, :, None]
    qc = np.arange(64)[None, None, None, :]
    drow = a + 7 - u
    valid = (np.abs(drow) <= 7)
    dri = np.clip(drow + 7, 0, 14)
    dci = np.clip(kc - qc, -15, 15) + 15
    dri_b = np.broadcast_to(dri, (2, 64, 16, 64))
    dci_b = np.broadcast_to(dci, (2, 64, 16, 64))
    g = rpb[:, :, dri_b, dci_b]
    g = np.where(np.broadcast_to(valid, (2, 64, 16, 64))[None, None], g, np.float32(0))
    g = np.ascontiguousarray(g.transpose(0, 2, 3, 1, 4, 5)).reshape(L, 128, 8 * 16 * 64).astype(np.float32)
    c0 = np.clip(qc - 8, 0, 48)
    colmask = (kc >= c0) & (kc < c0 + 16)
    m_full = np.broadcast_to(colmask & valid, (2, 64, 16, 64))
    m_int = np.broadcast_to(colmask & (drow >= -4) & (drow <= 3), (2, 64, 16, 64))
    m01 = np.stack([m_full, m_int], axis=2).astype(np.float32)
    return g, np.ascontiguousarray(m01).reshape(128, 2 * 16 * 64)


def host_shared(inp, layers):
    f = lambda k: np.asarray(inp[k], dtype=np.float32)
    ls = list(layers)
    sel = lambda k: np.ascontiguousarray(f(k)[ls])
    nl = len(ls)
    b_mod = sel("b_mod")
    fm = np.concatenate([b_mod.reshape(nl, 48, 128).transpose(0, 2, 1),
                         sel("g_norm1").reshape(nl, 8, 128).transpose(0, 2, 1),
                         sel("g_norm2").reshape(nl, 8, 128).transpose(0, 2, 1)], axis=2)
    rep = lambda v: np.broadcast_to(v[:, None, :], (nl, 128, v.shape[-1]))
    vecs = np.concatenate([rep(sel("na_q_gain")), rep(sel("na_k_gain")), rep(sel("gm_ln_g")), rep(sel("gm_ln_b")),
                           rep(sel("gm_b_s").reshape(nl, 512)), rep(sel("mla_cq_gain")), rep(sel("mla_ckv_gain")),
                           rep(sel("mla_q_gain")), rep(sel("mla_k_gain"))], axis=2)
    assert vecs.shape[2] == NVEC
    rpbg, m01 = _na_tables(sel("na_rpb"))
    wsT = sel("gm_w_s").transpose(0, 3, 1, 2).reshape(nl, 128, 512)
    sh = {
        "fm": np.ascontiguousarray(fm), "bmrow": np.ascontiguousarray(b_mod[:, None, :]),
        "vecs": np.ascontiguousarray(vecs), "w_mod": sel("w_mod"), "w_in": sel("w_in"),
        "rpbg": rpbg, "m01": m01, "ropet": _rope_table(), "wsT": np.ascontiguousarray(wsT),
        "w_uq": sel("mla_w_uq"), "w_ukv": sel("mla_w_ukv"), "na_w_o": sel("na_w_o"), "gm_w_o": sel("gm_w_o"),
        "mla_w_o": sel("mla_w_o"), "w_out": sel("w_out"), "ffn_w1": sel("ffn_w1"), "ffn_w2": sel("ffn_w2"),
    }
    return sh


def host_core(inp, b, xcur=None):
    f = lambda k: np.asarray(inp[k], dtype=np.float32)
    if xcur is None:
        xin = np.concatenate([f("ctx")[b], f("x")[b]], axis=0)
    else:
        xin = xcur
    cc = np.stack([f("c")[b], f("c_ctx")], axis=-1).reshape(8, 128, 2).transpose(1, 0, 2)
    return {"xin": np.ascontiguousarray(xin), "cc": np.ascontiguousarray(cc)}


FUSED = False
_PROGS = {}


def _prog(nl, flags):
    key = (nl, tuple(flags))
    if key not in _PROGS:
        _PROGS[key] = build(nl, list(flags))
    return _PROGS[key]


def kernel(**inputs):
    nb = int(np.asarray(inputs["x"]).shape[0])
    cores = list(range(nb))
    if FUSED:
        nc = _prog(DEPTH, [False] * (DEPTH - 1) + [True])
        sh = host_shared(inputs, range(DEPTH))
        in_maps = [{**sh, **host_core(inputs, b)} for b in range(nb)]
        res = run_bass_kernel_spmd(nc, in_maps, core_ids=cores)
        outs = [np.asarray(res.results[b]["out"]) for b in range(nb)]
    else:
        xcur = [None] * nb
        for l in range(DEPTH):
            nc = _prog(1, [False])
            sh = host_shared(inputs, [l])
            in_maps = [{**sh, **host_core(inputs, b, xcur[b])} for b in range(nb)]
            res = run_bass_kernel_spmd(nc, in_maps, core_ids=cores)
            xcur = [np.asarray(res.results[b]["out"], dtype=np.float32) for b in range(nb)]
        outs = xcur
    return np.stack([o[CL:] for o in outs]).astype(np.float32)
```
